# Optimizing a Trainium2 kernel written in Bass

```python
import jax, jax.numpy as jnp
from jax import lax
import numpy as np

D_MODEL = 2048
BATCH = 2
SEQ = 4096
DEPTH = 1

SB_HEADS = 8
SB_HEAD_DIM = 128
SB_WIDTH = SB_HEADS * SB_HEAD_DIM
SB_BLOCK = 128

SSD_D_INNER = 2048
SSD_HEAD_DIM = 64
SSD_HEADS = SSD_D_INNER // SSD_HEAD_DIM
SSD_GROUPS = 8
SSD_HEADS_PER_GROUP = SSD_HEADS // SSD_GROUPS
SSD_STATE = 128
SSD_CONV = 4
SSD_CHUNK = 256
SSD_CONV_CH = SSD_D_INNER + 2 * SSD_GROUPS * SSD_STATE

MEM_LEN = 256
MEM_HEADS = 4
MEM_HEAD_DIM = 256
MEM_WIDTH = MEM_HEADS * MEM_HEAD_DIM

N_BRANCHES = 3
NORM_EPS = 1e-6

IN_SPLITS = (SB_WIDTH, SB_WIDTH, SB_WIDTH, SB_WIDTH,
             SSD_D_INNER, SSD_CONV_CH, SSD_HEADS,
             MEM_WIDTH, MEM_WIDTH,
             N_BRANCHES * D_MODEL)
IN_WIDTH = sum(IN_SPLITS)

kernel_name = 'hybrid_stickbreak_ssd_memory_gated'


def rms_norm(x, w):
    xf = x.astype(jnp.float32)
    y = xf * lax.rsqrt(jnp.mean(xf * xf, axis=-1, keepdims=True) + NORM_EPS)
    return (y * w.astype(jnp.float32)).astype(x.dtype)


def gated_group_rms_norm(y, z, w):
    lead = y.shape[:-1]
    g = y.astype(jnp.float32) * jax.nn.silu(z.astype(jnp.float32))
    g = g.reshape(lead + (SSD_GROUPS, SSD_D_INNER // SSD_GROUPS))
    g = g * lax.rsqrt(jnp.mean(g * g, axis=-1, keepdims=True) + NORM_EPS)
    g = g.reshape(lead + (SSD_D_INNER,)) * w.astype(jnp.float32)
    return g.astype(z.dtype)


def stick_breaking_attention(q, k, v):
    b, s, h, d = q.shape
    nb = s // SB_BLOCK
    scale = d ** -0.5
    q_blocks = q.reshape(b, nb, SB_BLOCK, h, d).transpose(1, 0, 2, 3, 4)
    key_pos = jnp.arange(s)

    def one_block(args):
        q_blk, blk = args
        logits = jnp.einsum('bqhd,bkhd->bhqk', q_blk, k).astype(jnp.float32) * scale
        q_pos = blk * SB_BLOCK + jnp.arange(SB_BLOCK)
        earlier = key_pos[None, :] < q_pos[:, None]
        log_beta = jax.nn.log_sigmoid(logits)
        log_keep = jnp.where(earlier, log_beta - logits, 0.0)
        later = lax.cumsum(log_keep, axis=3, reverse=True) - log_keep
        weights = jnp.where(earlier, jnp.exp(log_beta + later), 0.0)
        return jnp.einsum('bhqk,bkhd->bqhd', weights.astype(v.dtype), v)

    out = lax.map(one_block, (q_blocks, jnp.arange(nb)))
    return out.transpose(1, 0, 2, 3, 4).reshape(b, s, h * d)


def causal_depthwise_conv(u, w, bias):
    ch = u.shape[-1]
    out = lax.conv_general_dilated(
        u, w[:, None, :].astype(u.dtype), window_strides=(1,),
        padding=[(SSD_CONV - 1, 0)], dimension_numbers=('NWC', 'WIO', 'NWC'),
        feature_group_count=ch)
    return out + bias.astype(u.dtype)


def ssd_chunked_scan(x, dt, a, b_in, c_in):
    bsz, s = x.shape[0], x.shape[1]
    pad = (-s) % SSD_CHUNK
    xs = (x.astype(jnp.float32) * dt[..., None])
    a_dt = dt * a
    bm = b_in.astype(jnp.float32)
    cm = c_in.astype(jnp.float32)
    xs = jnp.pad(xs, ((0, 0), (0, pad), (0, 0), (0, 0)))
    a_dt = jnp.pad(a_dt, ((0, 0), (0, pad), (0, 0)))
    bm = jnp.pad(bm, ((0, 0), (0, pad), (0, 0), (0, 0)))
    cm = jnp.pad(cm, ((0, 0), (0, pad), (0, 0), (0, 0)))
    nc = (s + pad) // SSD_CHUNK
    g, e, p, n, l = SSD_GROUPS, SSD_HEADS_PER_GROUP, SSD_HEAD_DIM, SSD_STATE, SSD_CHUNK
    xs = xs.reshape(bsz, nc, l, g, e, p)
    bm = bm.reshape(bsz, nc, l, g, n)
    cm = cm.reshape(bsz, nc, l, g, n)
    a_dt = a_dt.reshape(bsz, nc, l, g, e).transpose(0, 3, 4, 1, 2)
    a_cum = jnp.cumsum(a_dt, axis=-1)
    tri = jnp.tril(jnp.ones((l, l), dtype=bool))
    seg = a_cum[..., :, None] - a_cum[..., None, :]
    decay_in = jnp.exp(jnp.where(tri, seg, -jnp.inf))
    cb = jnp.einsum('bclgn,bcsgn->bcgls', cm, bm)
    y_diag = jnp.einsum('bcgls,bgecls,bcsgep->bclgep', cb, decay_in, xs)
    decay_to_end = jnp.exp(a_cum[..., -1:] - a_cum)
    states = jnp.einsum('bclgn,bgecl,bclgep->bcgepn', bm, decay_to_end, xs)
    states = jnp.concatenate([jnp.zeros_like(states[:, :1]), states], axis=1)
    chunk_cum = jnp.cumsum(jnp.pad(a_cum[..., -1], ((0, 0), (0, 0), (0, 0), (1, 0))), axis=-1)
    tri_c = jnp.tril(jnp.ones((nc + 1, nc + 1), dtype=bool))
    decay_chunk = jnp.exp(jnp.where(tri_c, chunk_cum[..., :, None] - chunk_cum[..., None, :], -jnp.inf))
    states = jnp.einsum('bgezc,bcgepn->bzgepn', decay_chunk, states)[:, :-1]
    y_off = jnp.einsum('bclgn,bcgepn,bgecl->bclgep', cm, states, jnp.exp(a_cum))
    y = (y_diag + y_off).reshape(bsz, nc * l, SSD_HEADS, p)
    return y[:, :s]


def memory_cross_attention(q, mem_n, w_kv):
    bsz, m = mem_n.shape[0], mem_n.shape[1]
    kv = mem_n @ w_kv
    k, v = jnp.split(kv, 2, axis=-1)
    k = k.reshape(bsz, m, MEM_HEADS, MEM_HEAD_DIM)
    v = v.reshape(bsz, m, MEM_HEADS, MEM_HEAD_DIM)
    logits = jnp.einsum('bqhd,bkhd->bhqk', q, k).astype(jnp.float32) * (MEM_HEAD_DIM ** -0.5)
    probs = jax.nn.softmax(logits, axis=-1)
    o = jnp.einsum('bhqk,bkhd->bqhd', probs.astype(v.dtype), v)
    return o.reshape(bsz, q.shape[1], MEM_WIDTH)


def setup_inputs(seed: int = 0) -> dict:
    key = jax.random.key(seed)
    ks = jax.random.split(key, 20)

    def dense(k, fan_in, fan_out):
        return jax.random.normal(k, (DEPTH, fan_in, fan_out), jnp.float32) * fan_in ** -0.5

    def gain(k, dim):
        return 1.0 + 0.02 * jax.random.normal(k, (DEPTH, dim), jnp.float32)

    dt0 = jnp.exp(jax.random.uniform(ks[8], (DEPTH, SSD_HEADS), jnp.float32,
                                     float(np.log(1e-3)), float(np.log(1e-1))))
    dt_bias = dt0 + jnp.log(-jnp.expm1(-dt0))
    a_log = jnp.log(jax.random.uniform(ks[9], (DEPTH, SSD_HEADS), jnp.float32, 1.0, 16.0))
    return {
        'x': jax.random.normal(ks[0], (BATCH, SEQ, D_MODEL), jnp.float32),
        'mem': jax.random.normal(ks[1], (BATCH, MEM_LEN, D_MODEL), jnp.float32),
        'norm_w': gain(ks[2], D_MODEL),
        'mem_norm_w': gain(ks[3], D_MODEL),
        'w_in': dense(ks[4], D_MODEL, IN_WIDTH),
        'b_gate': 0.01 * jax.random.normal(ks[5], (DEPTH, N_BRANCHES * D_MODEL), jnp.float32),
        'conv_w': jax.random.normal(ks[6], (DEPTH, SSD_CONV, SSD_CONV_CH), jnp.float32) * SSD_CONV ** -0.5,
        'conv_b': 0.01 * jax.random.normal(ks[7], (DEPTH, SSD_CONV_CH), jnp.float32),
        'dt_bias': dt_bias,
        'a_log': a_log,
        'd_skip': 1.0 + 0.1 * jax.random.normal(ks[10], (DEPTH, SSD_HEADS), jnp.float32),
        'ssd_norm_w': gain(ks[11], SSD_D_INNER),
        'w_mem_kv': dense(ks[12], D_MODEL, 2 * MEM_WIDTH),
        'w_branch_sb': dense(ks[13], SB_WIDTH, D_MODEL),
        'w_branch_ssd': dense(ks[14], SSD_D_INNER, D_MODEL),
        'w_branch_mem': dense(ks[15], MEM_WIDTH, D_MODEL),
        'w_out': dense(ks[16], D_MODEL, D_MODEL),
        'final_norm_w': 1.0 + 0.02 * jax.random.normal(ks[17], (D_MODEL,), jnp.float32),
    }


def reference(x, mem, norm_w, mem_norm_w, w_in, b_gate, conv_w, conv_b, dt_bias, a_log,
              d_skip, ssd_norm_w, w_mem_kv, w_branch_sb, w_branch_ssd, w_branch_mem,
              w_out, final_norm_w):
    bsz, s, _ = x.shape
    offsets = [int(o) for o in np.cumsum(np.array(IN_SPLITS))[:-1]]
    for layer in range(DEPTH):
        h = rms_norm(x, norm_w[layer])
        proj = h @ w_in[layer]
        (sb_q, sb_k, sb_v, sb_z, ssd_z, ssd_xbc, ssd_dt,
         mem_q, mem_z, gate_pre) = jnp.split(proj, offsets, axis=-1)

        o_sb = stick_breaking_attention(
            sb_q.reshape(bsz, s, SB_HEADS, SB_HEAD_DIM),
            sb_k.reshape(bsz, s, SB_HEADS, SB_HEAD_DIM),
            sb_v.reshape(bsz, s, SB_HEADS, SB_HEAD_DIM))
        o_sb = o_sb * jax.nn.silu(sb_z)

        xbc = jax.nn.silu(causal_depthwise_conv(ssd_xbc, conv_w[layer], conv_b[layer]))
        x_ssm, b_in, c_in = jnp.split(xbc, [SSD_D_INNER, SSD_D_INNER + SSD_GROUPS * SSD_STATE], axis=-1)
        dt = jax.nn.softplus(ssd_dt.astype(jnp.float32) + dt_bias[layer].astype(jnp.float32))
        a = -jnp.exp(a_log[layer].astype(jnp.float32))
        x_h = x_ssm.reshape(bsz, s, SSD_HEADS, SSD_HEAD_DIM)
        y = ssd_chunked_scan(x_h, dt, a,
                             b_in.reshape(bsz, s, SSD_GROUPS, SSD_STATE),
                             c_in.reshape(bsz, s, SSD_GROUPS, SSD_STATE))
        y = y + d_skip[layer].astype(jnp.float32)[:, None] * x_h.astype(jnp.float32)
        o_ssd = gated_group_rms_norm(y.reshape(bsz, s, SSD_D_INNER), ssd_z, ssd_norm_w[layer])

        mem_n = rms_norm(mem, mem_norm_w[layer])
        o_mem = memory_cross_attention(mem_q.reshape(bsz, s, MEM_HEADS, MEM_HEAD_DIM),
                                       mem_n, w_mem_kv[layer])
        o_mem = o_mem * jax.nn.silu(mem_z)

        gates = jax.nn.sigmoid((gate_pre + b_gate[layer]).astype(jnp.float32)).astype(x.dtype)
        g_sb, g_ssd, g_mem = jnp.split(gates, N_BRANCHES, axis=-1)
        merged = (g_sb * (o_sb @ w_branch_sb[layer])
                  + g_ssd * (o_ssd @ w_branch_ssd[layer])
                  + g_mem * (o_mem @ w_branch_mem[layer]))
        x = x + merged @ w_out[layer]
    return rms_norm(x, final_norm_w)
```

```python
import numpy as np
import concourse.bass as bass
import concourse.mybir as mybir
from contextlib import ExitStack

F32 = mybir.dt.float32
BF16 = mybir.dt.bfloat16
I32 = mybir.dt.int32
ALU = mybir.AluOpType
AF = mybir.ActivationFunctionType
AX = mybir.AxisListType


class Buf:
    __slots__ = ("w", "r")

    def __init__(self):
        self.w = None
        self.r = []


class V:
    __slots__ = ("ap", "bufs")

    def __init__(self, ap, bufs):
        self.ap = ap
        self.bufs = bufs if isinstance(bufs, (list, tuple)) else [bufs]


class Tile:
    def __init__(self, h):
        self.h = h
        self.b = {}

    def buf(self, key=0):
        if key not in self.b:
            self.b[key] = Buf()
        return self.b[key]

    def __call__(self, idx=None, key=0):
        ap = self.h[idx] if idx is not None else self.h[:]
        keys = key if isinstance(key, (list, tuple)) else [key]
        return V(ap, [self.buf(k) for k in keys])


class Eng:
    def __init__(self, name, h, sem):
        self.name = name
        self.h = h
        self.sem = sem
        self.cnt = 0
        self.known = {}


class Mk:
    def __init__(self, nc, es, n_dma_sems=24):
        self.nc = nc
        self.es = es
        mk = lambda n: es.enter_context(nc.semaphore(n))
        self.pe = Eng("pe", nc.tensor, mk("s_pe"))
        self.act = Eng("act", nc.scalar, mk("s_act"))
        self.dve = Eng("dve", nc.vector, mk("s_dve"))
        self.pool = Eng("pool", nc.gpsimd, mk("s_pool"))
        self.sp = Eng("sp", nc.sync, mk("s_sp"))
        self.dsems = [[mk("s_dma%d" % i), 0] for i in range(n_dma_sems)]
        self.dnext = 0
        self.qsems = {"sp": self.dsems[:16], "pool": self.dsems[16:]}
        self.qnext = {"sp": 0, "pool": 0}
        self.ccsem = [mk("s_cc"), 0]
        self.ntile = 0

    def sb(self, shape, dt, name=None):
        self.ntile += 1
        return Tile(self.es.enter_context(self.nc.sbuf_tensor("sb_" + (name or ("t%d" % self.ntile)), list(shape), dt)))

    def ps(self, shape, dt, name=None):
        self.ntile += 1
        return Tile(self.es.enter_context(self.nc.psum_tensor("ps_" + (name or ("p%d" % self.ntile)), list(shape), dt)))

    def wait(self, E, ev):
        if ev is None:
            return
        sem, val = ev
        k = id(sem)
        if E.known.get(k, 0) >= val:
            return
        if E is self.pe and sem is E.sem:
            return
        E.h.wait_ge(sem, val)
        E.known[k] = val

    def _deps(self, E, outs, ins):
        need = {}

        def add(ev):
            if ev is None:
                return
            k = id(ev[0])
            if k not in need or need[k][1] < ev[1]:
                need[k] = ev
        for v in ins:
            for b in v.bufs:
                add(b.w)
        for v in outs:
            for b in v.bufs:
                add(b.w)
                for r in b.r:
                    add(r)
        for ev in need.values():
            self.wait(E, ev)

    def _mark(self, ev, outs, ins):
        for v in ins:
            for b in v.bufs:
                b.r = [r for r in b.r if r[0] is not ev[0]] + [ev]
        for v in outs:
            for b in v.bufs:
                b.w = ev
                b.r = []

    def emit(self, E, fn, outs=(), ins=()):
        self._deps(E, outs, ins)
        inst = fn()
        E.cnt += 1
        inst.then_inc(E.sem, 1)
        ev = (E.sem, E.cnt)
        E.known[id(E.sem)] = max(E.known.get(id(E.sem), 0), 0)
        self._mark(ev, outs, ins)
        return ev

    def next_slot(self, Q):
        pool_ = self.qsems[Q.name]
        slot = pool_[self.qnext[Q.name] % len(pool_)]
        self.qnext[Q.name] += 1
        if slot[1] > 0:
            self.wait(Q, (slot[0], slot[1]))
        return slot

    def dma(self, Q, out_ap, in_ap, outs=(), ins=(), **kw):
        self._deps(Q, outs, ins)
        slot = self.next_slot(Q)
        inst = Q.h.dma_start(out=out_ap, in_=in_ap, **kw)
        slot[1] += 16
        inst.then_inc(slot[0], 16)
        ev = (slot[0], slot[1])
        self._mark(ev, outs, ins)
        return ev


    def barrier(self):
        engs = [self.pe, self.act, self.dve, self.pool, self.sp]
        evs = [(E.sem, E.cnt) for E in engs if E.cnt > 0]
        evs += [(s[0], s[1]) for s in self.dsems if s[1] > 0]
        if self.ccsem[1] > 0:
            evs.append((self.ccsem[0], self.ccsem[1]))
        for E in engs:
            for ev in evs:
                self.wait(E, ev)

    def activation(self, out, in_, func, bias=None, scale=1.0, accum=None, E=None):
        E = E or self.act
        ins = [in_]
        kw = {}
        if bias is not None:
            if isinstance(bias, V):
                ins.append(bias)
                kw["bias"] = bias.ap
            else:
                kw["bias"] = bias
        if isinstance(scale, V):
            ins.append(scale)
            kw["scale"] = scale.ap
        else:
            kw["scale"] = scale
        outs = [out]
        if accum is not None:
            outs.append(accum)
            kw["accum_out"] = accum.ap
        return self.emit(E, lambda: E.h.activation(out=out.ap, in_=in_.ap, func=func, **kw), outs, ins)

    def tt(self, out, a, b, op, E=None):
        E = E or self.dve
        return self.emit(E, lambda: E.h.tensor_tensor(out=out.ap, in0=a.ap, in1=b.ap, op=op), [out], [a, b])

    def ts(self, out, a, s1, op0, s2=None, op1=None, E=None, accum=None):
        E = E or self.dve
        ins = [a]
        x1 = s1
        if isinstance(s1, V):
            ins.append(s1)
            x1 = s1.ap
        x2 = s2
        if isinstance(s2, V):
            ins.append(s2)
            x2 = s2.ap
        kw = {}
        if op1 is not None:
            kw["op1"] = op1
        outs = [out]
        if accum is not None:
            outs.append(accum)
            kw["accum_out"] = accum.ap
        return self.emit(E, lambda: E.h.tensor_scalar(out=out.ap, in0=a.ap, scalar1=x1, scalar2=x2, op0=op0, **kw), outs, ins)

    def stt(self, out, in0, scalar, in1, op0, op1, E=None):
        E = E or self.dve
        ins = [in0, in1]
        sc = scalar
        if isinstance(scalar, V):
            ins.append(scalar)
            sc = scalar.ap
        return self.emit(E, lambda: E.h.scalar_tensor_tensor(out=out.ap, in0=in0.ap, scalar=sc, in1=in1.ap, op0=op0, op1=op1), [out], ins)

    def copy(self, out, in_, E=None):
        E = E or self.dve
        if E is self.act:
            return self.emit(E, lambda: E.h.activation(out=out.ap, in_=in_.ap, func=AF.Copy), [out], [in_])
        return self.emit(E, lambda: E.h.tensor_copy(out=out.ap, in_=in_.ap), [out], [in_])

    def memset(self, out, val, E=None):
        E = E or self.dve
        return self.emit(E, lambda: E.h.memset(out.ap, val), [out], [])

    def recip(self, out, in_, E=None):
        E = E or self.dve
        return self.emit(E, lambda: E.h.reciprocal(out=out.ap, in_=in_.ap), [out], [in_])

    def reduce(self, out, in_, op, axis=None, E=None):
        E = E or self.dve
        axis = axis or AX.X
        return self.emit(E, lambda: E.h.tensor_reduce(out=out.ap, in_=in_.ap, axis=axis, op=op), [out], [in_])

    def mm(self, out, items):
        ins = []
        for l, r in items:
            ins += [l, r]
        n = len(items)

        def fn():
            inst = None
            for i, (l, r) in enumerate(items):
                inst = self.nc.tensor.matmul(out.ap, l.ap, r.ap, start=(i == 0), stop=(i == n - 1))
            return inst
        return self.emit(self.pe, fn, [out], ins)

    def mm1(self, out, l, r, start, stop):
        return self.emit(self.pe, lambda: self.nc.tensor.matmul(out.ap, l.ap, r.ap, start=start, stop=stop), [out], [l, r])

    def transpose(self, out, in_, ident):
        return self.emit(self.pe, lambda: self.nc.tensor.transpose(out.ap, in_.ap, ident.ap), [out], [in_, ident])

from concourse.bass_utils import run_bass_kernel_spmd

S_ = np.s_
NCORES = 8
D = 2048
KC = 16
TOK = 8192
NTILE = 16
WA_COLS = 1284
EPS = 1e-6
SB_SCALE = 128.0 ** -0.5
MEM_SCALE = 256.0 ** -0.5
A_ROWS = 384


def bc_last(ap, shape):
    if hasattr(ap, "unsqueeze"):
        return ap.unsqueeze(2).broadcast_to(shape)
    return ap[:, :, None].broadcast_to(shape)


def build(debug_a=False, ntile=NTILE, tiles=None, cut=99, cut_all=99, stop=None):
    nc = bass.Bass("TRN2", target_bir_lowering=False)
    dr = lambda n, s, dt=F32: nc.dram_tensor(n, list(s), dt, kind="ExternalInput").ap()
    x_ds = [dr("x0", [4096, D]), dr("x1", [4096, D])]
    xo_d = dr("xo", [1024, D])
    mem_d = dr("mem", [256, D])
    wa_d = dr("wa", [128, KC, WA_COLS])
    wq_d = dr("wq", [128, KC, 2048])
    wkv_d = dr("wkv", [128, KC, 2048])
    wf_ds = [dr("wf%d" % i, [128, 80, 128]) for i in range(16)]
    wo_d = dr("wo", [128, KC, 2048])
    va_d = dr("va", [128, 304])
    vc_d = dr("vc", [128, 80])
    fnw_d = dr("fnw", [128, D])
    cst_d = dr("cst", [128, 6 * 128])
    gidx_d = dr("gidx", [128, 2], I32)
    if debug_a:
        y_d = nc.dram_tensor("y", [NCORES * A_ROWS, 1024], BF16, kind="ExternalOutput").ap()
    else:
        y_d = nc.dram_tensor("y", [1024, D], F32, kind="ExternalOutput").ap()
    a_q = [nc.dram_tensor("a_q%d" % q, [NCORES * 32, 1024], BF16) for q in range(12)]
    G_q = [nc.dram_tensor("G_q%d" % q, [NCORES * NCORES * 32, 1024], BF16) for q in range(12)]

    with ExitStack() as es:
        M = Mk(nc, es)
        cst = M.sb([128, 768], F32, "cst")
        identb = M.sb([128, 128], BF16, "identb")
        negtrib = M.sb([128, 128], BF16, "negtrib")
        negoneb = M.sb([128, 128], BF16, "negoneb")
        maskUb = M.sb([128, 128], BF16, "maskUb")
        one_col = M.sb([128, 1], F32, "one_col")
        eps_col = M.sb([128, 1], F32, "eps_col")
        PB = [M.ps([128, 512], F32, "pb%d" % i) for i in range(8)]
        pstate = [0]

        def psn():
            b = PB[pstate[0] % 7]
            pstate[0] += 1
            return b

        def pbf(bank, a, b):
            return V(bank.h.bitcast(BF16)[:, a:b], bank.buf())

        M.dma(M.sp, cst.h[:], cst_d[:, :], outs=[cst()])
        trif = cst(S_[:, 128:256])
        onesf = cst(S_[:, 256:384])
        M.copy(identb(), cst(S_[:, 0:128]))
        M.copy(negtrib(), cst(S_[:, 384:512]))
        M.copy(negoneb(), cst(S_[:, 512:640]))
        M.copy(maskUb(), cst(S_[:, 640:768]))
        M.memset(one_col(), 1.0)
        M.memset(eps_col(), EPS)

        def silu_to(out, src, tmp):
            M.activation(tmp, src, AF.Exp, scale=-1.0)
            M.ts(tmp, tmp, 1.0, ALU.add)
            M.recip(tmp, tmp)
            M.tt(out, src, tmp, ALU.mult)

        def rstd_of(out, ssq, n, tmp):
            M.activation(tmp, ssq, AF.Ln, bias=eps_col(), scale=1.0 / n)
            M.activation(out, tmp, AF.Exp, scale=-0.5)

        a_in_bufs = []

        with ExitStack() as esa:
            M.es = esa
            WA = M.sb([128, KC, WA_COLS], BF16, "WA")
            va = M.sb([128, 304], F32, "va")
            xbuf = M.sb([128, 4, D], F32, "xbuf")
            junk = M.sb([128, D], BF16, "junk")
            hbf = M.sb([128, 4, D], BF16, "hbf")
            hT = M.sb([128, KC, 512], BF16, "hT")
            KT = M.sb([128, 4096], BF16, "KT")
            Vt = M.sb([128, 32, 128], BF16, "Vt")
            qT = M.sb([128, 512], BF16, "qT")
            zsbT = M.sb([128, 512], BF16, "zsbT")
            U = [M.sb([128, 515], F32, "U%d" % i) for i in range(4)]
            cacc = M.sb([128, 512], F32, "cacc")
            ctmp = M.sb([128, 512], F32, "ctmp")
            convo = [M.sb([128, 512], BF16, "convo%d" % i) for i in range(4)]
            ssq = M.sb([128, 4], F32, "ssq")
            lnv = M.sb([128, 4], F32, "lnv")
            rstd = M.sb([128, 4], F32, "rstd")
            Esb = [M.sb([128, 512], F32, "Esb%d" % i) for i in range(2)]
            SPb = [M.sb([128, 512], BF16, "SPb%d" % i) for i in range(2)]
            Wb = [M.sb([128, 512], BF16, "Wb%d" % i) for i in range(2)]
            Wfirst = M.sb([128, 512], BF16, "Wfirst")
            Acc = M.sb([128, 512], BF16, "Acc")
            osbT = M.sb([128, 512], BF16, "osbT")
            ossdT = M.sb([128, 2, 512], BF16, "ossdT")
            xBtok = M.sb([128, 4, 384], BF16, "xBtok")
            sztok = M.sb([128, 4, 256], F32, "sztok")
            ztmp = M.sb([128, 256], F32, "ztmp")
            dtr = M.sb([128, 16], F32, "dtr")
            dts = M.sb([128, 16], F32, "dts")
            adt = M.sb([128, 16], F32, "adt")
            a_bc = M.sb([128, 16], F32, "a_bc")
            ac = M.sb([128, 12], F32, "ac")
            ex3 = M.sb([128, 12], F32, "ex3")
            dtd = M.sb([128, 4], F32, "dtd")
            adtb = [M.sb([128, 128], F32, "adtb%d" % i) for i in range(4)]
            darg = M.sb([128, 512], F32, "darg")
            Dm = M.sb([128, 512], F32, "Dm")
            CBm = M.sb([128, 128], F32, "CBm")
            Mt = M.sb([128, 512], BF16, "Mt")
            xs_bf = M.sb([128, 256], BF16, "xs_bf")
            xsd = M.sb([128, 256], BF16, "xsd")
            Sst = M.sb([128, 256], F32, "Sst")
            Sbf = M.sb([128, 256], BF16, "Sbf")
            t1 = M.sb([128, 256], F32, "t1")
            t2 = M.sb([128, 256], F32, "t2")
            t3 = M.sb([128, 256], F32, "t3")
            gy = M.sb([128, 256], F32, "gy")
            gjunk = M.sb([128, 256], BF16, "gjunk")
            gss = M.sb([128, 1], F32, "gss")
            gln = M.sb([128, 1], F32, "gln")
            grs = M.sb([128, 1], F32, "grs")
            obf = M.sb([128, 256], BF16, "obf")

            for kc in range(KC):
                M.dma(M.pool, WA.h[:, kc, :], wa_d[:, kc, :], outs=[WA(key=kc)])
            WAk = lambda kc, a, b: WA(S_[:, kc, a:b], key=kc)
            M.dma(M.sp, va.h[:], va_d[:, :], outs=[va()])
            nwc = lambda kc: va(S_[:, kc:kc + 1])
            cw = lambda ch, k: va(S_[:, 16 + ch * 4 + k:16 + ch * 4 + k + 1])
            cb = lambda ch: va(S_[:, 32 + ch:33 + ch])
            dtb = va(S_[:, 36:40])
            alog = va(S_[:, 40:44])
            dsk = va(S_[:, 44:48])
            snw = va(S_[:, 48:304])
            for i in range(4):
                M.activation(a_bc(S_[:, i * 4:(i + 1) * 4]), alog, AF.Exp)
            M.ts(a_bc(), a_bc(), -1.0, ALU.mult)
            M.memset(Wfirst(), 0.0)

            tile_list = list(range(ntile)) if tiles is None else tiles
            for ti, T in enumerate(tile_list):
                b_, g = divmod(T, 8)
                lastcut = cut if ti == len(tile_list) - 1 else cut_all
                M.dma(M.sp, xbuf.h[:], x_ds[b_][g * 512:(g + 1) * 512, :].rearrange("(s p) d -> p s d", p=128),
                      outs=[xbuf()])
                for s in range(4):
                    M.activation(junk(), xbuf(S_[:, s, :]), AF.Square, accum=ssq(S_[:, s:s + 1]))
                rstd_of(rstd(), ssq(), D, lnv())
                for s in range(4):
                    M.ts(hbf(S_[:, s, :], key=s), xbuf(S_[:, s, :]), rstd(S_[:, s:s + 1]), ALU.mult)
                if lastcut < 2:
                    continue
                for kc in range(KC):
                    bank = psn()
                    for s in range(4):
                        M.transpose(pbf(bank, s * 128, (s + 1) * 128), hbf(S_[:, s, kc * 128:(kc + 1) * 128], key=s),
                                    identb())
                    M.ts(hT(S_[:, kc, :], key=kc), pbf(bank, 0, 512), nwc(kc), ALU.mult)

                if lastcut < 3:
                    continue

                def proj_fm(c0):
                    bank = psn()
                    M.mm(bank(), [(WAk(kc, c0, c0 + 128), hT(S_[:, kc, :], key=kc)) for kc in range(KC)])
                    return bank

                def proj_tm(s, c0, n):
                    bank = psn()
                    M.mm(bank(S_[:, 0:n]), [(hT(S_[:, kc, s * 128:(s + 1) * 128], key=kc), WAk(kc, c0, c0 + n))
                                           for kc in range(KC)])
                    return bank

                bank = proj_fm(0)
                M.activation(qT(), bank(), AF.Copy, scale=SB_SCALE)
                bank = proj_fm(128)
                M.copy(KT(S_[:, g * 512:(g + 1) * 512], key=g), bank())
                bank = proj_fm(256)
                silu_to(zsbT(), bank(), ctmp())
                for ch in range(4):
                    bank = proj_fm(384 + ch * 128)
                    if g == 0:
                        M.memset(U[ch](S_[:, 0:3]), 0.0)
                    M.copy(U[ch](S_[:, 3:515]), bank(), E=M.act)
                    M.ts(cacc(), U[ch](S_[:, 0:512]), cw(ch, 0), ALU.mult, cb(ch), ALU.add)
                    for k in range(1, 4):
                        M.stt(cacc(), U[ch](S_[:, k:k + 512]), cw(ch, k), cacc(), ALU.mult, ALU.add)
                    M.copy(U[ch](S_[:, 0:3]), U[ch](S_[:, 512:515]))
                    silu_to(convo[ch](), cacc(), ctmp())
                for s in range(4):
                    bank = proj_tm(s, 896, 388)
                    M.copy(Vt(S_[:, g * 4 + s, :], key=g), bank(S_[:, 0:128]), E=M.act)
                    silu_to(sztok(S_[:, s, :], key=s), bank(S_[:, 128:384]), ztmp())
                    M.tt(dtr(S_[:, s * 4:(s + 1) * 4]), bank(S_[:, 384:388]), dtb, ALU.add)
                M.activation(dtr(), dtr(), AF.Exp)
                M.activation(dts(), dtr(), AF.Ln, bias=one_col())
                M.tt(adt(), dts(), a_bc(), ALU.mult)

                if lastcut < 4:
                    continue
                M.memset(Acc(), 0.0)
                oT = PB[7]
                nkb = 4 * g + 4
                blocks = list(range(nkb - 1, -1, -1))

                def sb_stage1(kb):
                    r = kb - 4 * g
                    c0 = max(r, 0) * 128
                    sl = S_[:, c0:512]
                    par = kb % 2
                    ktb = KT(S_[:, kb * 128:(kb + 1) * 128], key=kb // 4)
                    bA = psn()
                    M.mm(bA(sl), [(ktb, qT(sl))])
                    M.activation(Esb[par](sl), bA(sl), AF.Exp)
                    M.activation(SPb[par](sl), Esb[par](sl), AF.Ln, bias=one_col())
                    if r >= 0:
                        dsl = S_[:, c0:c0 + 128]
                        M.tt(SPb[par](dsl), SPb[par](dsl), maskUb(), ALU.mult)

                def sb_stage2(kb, first):
                    r = kb - 4 * g
                    c0 = max(r, 0) * 128
                    sl = S_[:, c0:512]
                    dsl = S_[:, c0:c0 + 128]
                    par = kb % 2
                    ktb = KT(S_[:, kb * 128:(kb + 1) * 128], key=kb // 4)
                    bB = psn()
                    items = [(ktb, qT(sl)), (negtrib(), SPb[par](sl))]
                    if not first:
                        items.append((negoneb(), Acc(sl)))
                    M.mm(bB(sl), items)
                    wt = Wfirst if first else Wb[par]
                    M.activation(wt(sl), bB(sl), AF.Exp)
                    if r >= 0:
                        M.tt(wt(dsl), wt(dsl), maskUb(), ALU.mult)
                    osl = S_[:, 0:512] if first else sl
                    M.mm1(oT(osl), Vt(S_[:, kb, :], key=kb // 4), wt(osl), start=first, stop=(kb == 0))
                    if kb > 0:
                        M.tt(Acc(sl), Acc(sl), SPb[par](sl), ALU.add)

                sb_stage1(blocks[0])
                for bi, kb in enumerate(blocks):
                    if bi + 1 < len(blocks):
                        sb_stage1(blocks[bi + 1])
                    sb_stage2(kb, bi == 0)
                M.tt(osbT(), oT(), zsbT(), ALU.mult)
                dest, off = divmod(T, 2)
                off *= 512
                for q in range(4):
                    bsb = Buf()
                    a_in_bufs.append(bsb)
                    M.dma(M.sp, a_q[q].ap()[dest * 32:(dest + 1) * 32, off:off + 512], osbT.h[q * 32:(q + 1) * 32, :],
                          outs=[V(None, bsb)], ins=[osbT()])

                if lastcut < 4.5:
                    continue
                if g == 0 and lastcut != 4.5:
                    M.memset(Sst(), 0.0)
                    M.memset(Sbf(), 0.0)
                for s in range(4 if lastcut != 4.5 else 0):
                    bank = psn()
                    for ch in range(3):
                        M.transpose(pbf(bank, ch * 128, (ch + 1) * 128), convo[ch](S_[:, s * 128:(s + 1) * 128]),
                                    identb())
                    M.copy(xBtok(S_[:, s, :], key=s), pbf(bank, 0, 384), E=M.act)
                for s in range(4):
                    if lastcut < 5.1:
                        continue
                    tsl = S_[:, s * 128:(s + 1) * 128]
                    adt_s = adt(S_[:, s * 4:(s + 1) * 4])
                    dt_s = dts(S_[:, s * 4:(s + 1) * 4])
                    bank = psn()
                    M.mm(bank(S_[:, 0:4]), [(trif, adt_s)])
                    M.mm(bank(S_[:, 4:8]), [(onesf, adt_s)])
                    M.copy(ac(S_[:, 0:8]), bank(S_[:, 0:8]))
                    M.tt(ac(S_[:, 8:12]), ac(S_[:, 4:8]), ac(S_[:, 0:4]), ALU.subtract)
                    M.activation(ex3(), ac(), AF.Exp)
                    ecum = ex3.h[:, 0:4]
                    etot = ex3.h[:, 4:8]
                    M.tt(dtd(), dt_s, ex3(S_[:, 8:12]), ALU.mult)
                    x3 = V(xBtok.h[:, s, 0:256].rearrange("p (e q) -> p e q", e=4), xBtok.buf(s))
                    M.tt(V(xs_bf.h[:, :].rearrange("p (e q) -> p e q", e=4), xs_bf.buf()), x3,
                         V(bc_last(dts.h[:, s * 4:(s + 1) * 4], [128, 4, 64]), dts.buf()), ALU.mult)
                    M.tt(V(xsd.h[:, :].rearrange("p (e q) -> p e q", e=4), xsd.buf()), x3,
                         V(bc_last(dtd.h[:, 0:4], [128, 4, 64]), dtd.buf()), ALU.mult)
                    if lastcut < 5.2:
                        continue
                    bR = psn()
                    for e in range(4):
                        M.ts(adtb[e](), onesf, adt(S_[:, s * 4 + e:s * 4 + e + 1]), ALU.mult)
                        M.mm(bR(S_[:, e * 128:(e + 1) * 128]), [(adtb[e](), trif)])
                    for e in range(4):
                        M.ts(darg(S_[:, e * 128:(e + 1) * 128]), bR(S_[:, e * 128:(e + 1) * 128]),
                             ac(S_[:, e:e + 1]), ALU.subtract, 0.0, ALU.min)
                    M.activation(Dm(), darg(), AF.Exp)
                    if lastcut < 5.3:
                        continue
                    bC = psn()
                    M.mm(bC(S_[:, 0:128]), [(convo[2](tsl), convo[3](tsl))])
                    M.tt(CBm(), bC(S_[:, 0:128]), trif, ALU.mult)
                    M.tt(V(Mt.h[:, :].rearrange("p (e q) -> p e q", e=4), Mt.buf()),
                         V(Dm.h[:, :].rearrange("p (e q) -> p e q", e=4), Dm.buf()),
                         V(CBm.h[:, None, :].broadcast_to([128, 4, 128]), CBm.buf()), ALU.mult)
                    if lastcut < 5.4:
                        continue
                    bY = psn()
                    for e in range(4):
                        M.mm(bY(S_[:, e * 64:(e + 1) * 64]),
                             [(Mt(S_[:, e * 128:(e + 1) * 128]), xs_bf(S_[:, e * 64:(e + 1) * 64]))])
                    M.mm(bY(S_[:, 256:512]), [(convo[3](tsl), Sbf())])
                    M.tt(V(t1.h[:, :].rearrange("p (e q) -> p e q", e=4), t1.buf()),
                         V(bY.h[:, 256:512].rearrange("p (e q) -> p e q", e=4), bY.buf()),
                         V(bc_last(ecum, [128, 4, 64]), ex3.buf()), ALU.mult)
                    M.tt(t2(), bY(S_[:, 0:256]), t1(), ALU.add)
                    M.tt(V(t3.h[:, :].rearrange("p (e q) -> p e q", e=4), t3.buf()), x3,
                         V(bc_last(va.h[:, 44:48], [128, 4, 64]), va.buf()), ALU.mult)
                    M.tt(t2(), t2(), t3(), ALU.add)
                    if lastcut < 5.5:
                        continue
                    M.tt(gy(), t2(), sztok(S_[:, s, :], key=s), ALU.mult)
                    M.activation(gjunk(), gy(), AF.Square, accum=gss())
                    rstd_of(grs(), gss(), 256, gln())
                    M.stt(obf(), gy(), grs(), snw, ALU.mult, ALU.mult)
                    bank = psn()
                    for ch in range(2):
                        M.transpose(pbf(bank, ch * 128, (ch + 1) * 128), obf(S_[:, ch * 128:(ch + 1) * 128]), identb())
                    M.copy(V(ossdT.h[:, :, s * 128:(s + 1) * 128], ossdT.buf()),
                           V(bank.h.bitcast(BF16)[:, 0:256].rearrange("p (c q) -> p c q", c=2), bank.buf()), E=M.act)
                    if lastcut < 5.6:
                        continue
                    bL = psn()
                    M.mm(bL(S_[:, 0:256]), [(xBtok(S_[:, s, 256:384], key=s), xsd())])
                    M.tt(V(Sst.h[:, :].rearrange("p (e q) -> p e q", e=4), Sst.buf()),
                         V(Sst.h[:, :].rearrange("p (e q) -> p e q", e=4), Sst.buf()),
                         V(bc_last(etot, [128, 4, 64]), ex3.buf()), ALU.mult)
                    M.tt(Sst(), Sst(), bL(S_[:, 0:256]), ALU.add)
                    M.copy(Sbf(), Sst(), E=M.act)
                for q in range(8):
                    bss = Buf()
                    a_in_bufs.append(bss)
                    M.dma(M.sp, a_q[4 + q].ap()[dest * 32:(dest + 1) * 32, off:off + 512],
                          ossdT.h[(q % 4) * 32:(q % 4 + 1) * 32, q // 4, :], outs=[V(None, bss)], ins=[ossdT()])
            M.barrier()
        M.es = es

        if debug_a:
            M._deps(M.sp, [], [V(None, b) for b in a_in_bufs])
            for q in range(12):
                ev = M.dma(M.sp, y_d[q * 256:(q + 1) * 256, :], a_q[q].ap()[:, :])
                M.wait(M.sp, ev)
            return nc

        Gbuf = Buf()
        M._deps(M.pool, [V(None, Gbuf)], [V(None, b) for b in a_in_bufs])
        pending_ag = list(zip(a_q, G_q))

        def emit_ag(n):
            for _ in range(n):
                if not pending_ag:
                    return
                a_t, g_t = pending_ag.pop(0)
                inst = nc.gpsimd.collective_compute("AllGather", ALU.bypass, replica_groups=[list(range(NCORES))],
                                                    ins=[a_t.ap().opt()], outs=[g_t.ap().opt()])
                M.ccsem[1] += 1
                inst.then_inc(M.ccsem[0], 1)
                Gbuf.w = (M.ccsem[0], M.ccsem[1])
        emit_ag(12)
        if stop == "ag":
            M.barrier()
            return nc

        with ExitStack() as esc:
            M.es = esc
            mergedT = M.sb([128, 16, 1024], BF16, "mergedT")
            vc = M.sb([128, 80], F32, "vc")
            negbg = M.sb([128, 48], F32, "negbg")
            gidx = M.sb([128, 2], I32, "gidx")
            M.dma(M.sp, vc.h[:], vc_d[:, :], outs=[vc()])
            M.dma(M.sp, gidx.h[:], gidx_d[:, :], outs=[gidx()])
            M.ts(negbg(), vc(S_[:, 0:48]), -1.0, ALU.mult)
            nwc2 = lambda kc: vc(S_[:, 48 + kc:49 + kc])
            mnwc = lambda kc: vc(S_[:, 64 + kc:65 + kc])

            es12 = esc.enter_context(ExitStack())
            M.es = es12
            hTo = M.sb([128, KC, 1024], BF16, "hTo")
            omT = M.sb([128, 8, 1024], BF16, "omT")
            memT = M.sb([128, KC, 256], BF16, "memT")
            with ExitStack() as es1a:
                M.es = es1a
                xb2 = M.sb([128, 2, D], F32, "xb2")
                junk2 = M.sb([128, D], BF16, "junk2")
                hb2 = M.sb([128, 2, D], BF16, "hb2")
                ssq2 = M.sb([128, 4], F32, "ssq2")
                lnv2 = M.sb([128, 4], F32, "lnv2")
                rstd2 = M.sb([128, 4], F32, "rstd2")
                def norm_T(src_d, nsub, dstT, col0, nw):
                    M.dma(M.sp, xb2.h[:, 0:nsub, :], src_d.rearrange("(s p) d -> p s d", p=128), outs=[xb2()])
                    for s in range(nsub):
                        M.activation(junk2(), xb2(S_[:, s, :]), AF.Square, accum=ssq2(S_[:, s:s + 1]))
                    rstd_of(rstd2(S_[:, 0:nsub]), ssq2(S_[:, 0:nsub]), D, lnv2(S_[:, 0:nsub]))
                    for s in range(nsub):
                        M.ts(hb2(S_[:, s, :], key=s), xb2(S_[:, s, :]), rstd2(S_[:, s:s + 1]), ALU.mult)
                    for kc in range(KC):
                        bank = psn()
                        for s in range(nsub):
                            M.transpose(pbf(bank, s * 128, (s + 1) * 128),
                                        hb2(S_[:, s, kc * 128:(kc + 1) * 128], key=s), identb())
                        M.ts(dstT(S_[:, kc, col0:col0 + nsub * 128], key=kc), pbf(bank, 0, nsub * 128), nw(kc),
                             ALU.mult)

                for grp in range(4):
                    norm_T(xo_d[grp * 256:(grp + 1) * 256, :], 2, hTo, grp * 256, nwc2)
                norm_T(mem_d[:, :], 2, memT, 0, mnwc)
                M.barrier()

            with ExitStack() as es1:
                M.es = es1
                kTs = M.sb([128, 8, 256], BF16, "kTs")
                vsb = M.sb([128, 2, 1024], BF16, "vsb")
                wblk = [M.sb([128, KC, 512], BF16, "wblk%d" % i) for i in range(2)]
                qmT = M.sb([128, 8, 1024], BF16, "qmT")
                szmT = M.sb([128, 8, 1024], BF16, "szmT")
                stmp = M.sb([128, 512], F32, "stmp")
                mx = M.sb([128, 4], F32, "mx")
                nmx = M.sb([128, 4], F32, "nmx")
                rsum = M.sb([128, 4], F32, "rsum")
                rinv = M.sb([128, 4], F32, "rinv")
                pf = M.sb([128, 4, 256], F32, "pf")
                pn = M.sb([128, 4, 256], BF16, "pn")
                pT = M.sb([128, 1024], BF16, "pT")

                nblk = [0]

                def load_blk(src_d, blk):
                    t = wblk[nblk[0] % 2]
                    nblk[0] += 1
                    for q4 in range(4):
                        M.dma(M.pool, t.h[:, q4 * 4:(q4 + 1) * 4, :],
                              src_d[:, q4 * 4:(q4 + 1) * 4, blk * 512:(blk + 1) * 512], outs=[t(key=q4)])
                    emit_ag(2)
                    return t

                wk = lambda t, kc, a, b: t(S_[:, kc, a:b], key=kc // 4)
                for blk in range(2):
                    t = load_blk(wkv_d, blk)
                    for cc in range(4):
                        bank = psn()
                        M.mm(bank(S_[:, 0:256]), [(wk(t, kc, cc * 128, (cc + 1) * 128), memT(S_[:, kc, :], key=kc))
                                                  for kc in range(KC)])
                        M.copy(kTs(S_[:, blk * 4 + cc, :]), bank(S_[:, 0:256]), E=M.act)
                for blk in range(2):
                    t = load_blk(wkv_d, 2 + blk)
                    for mt in range(2):
                        bank = psn()
                        M.mm(bank(), [(memT(S_[:, kc, mt * 128:(mt + 1) * 128], key=kc), wk(t, kc, 0, 512))
                                      for kc in range(KC)])
                        M.copy(vsb(S_[:, mt, blk * 512:(blk + 1) * 512]), bank(), E=M.act)
                for blk in range(4):
                    t = load_blk(wq_d, blk)
                    for cc in range(4):
                        for grp in range(2):
                            gsl = S_[:, grp * 512:(grp + 1) * 512]
                            bank = psn()
                            M.mm(bank(), [(wk(t, kc, cc * 128, (cc + 1) * 128),
                                           hTo(S_[:, kc, grp * 512:(grp + 1) * 512], key=kc)) for kc in range(KC)])
                            if blk < 2:
                                M.activation(V(qmT.h[:, blk * 4 + cc, grp * 512:(grp + 1) * 512], qmT.buf()), bank(),
                                             AF.Copy, scale=MEM_SCALE)
                            else:
                                silu_to(V(szmT.h[:, (blk - 2) * 4 + cc, grp * 512:(grp + 1) * 512], szmT.buf()),
                                        bank(), stmp())
                for qt in range(8):
                    qsl = slice(qt * 128, (qt + 1) * 128)
                    lb = [psn(), psn()]
                    for h in range(4):
                        M.mm(lb[h // 2](S_[:, (h % 2) * 256:(h % 2 + 1) * 256]),
                             [(V(qmT.h[:, 2 * h + dc, qsl], qmT.buf()), kTs(S_[:, 2 * h + dc, :])) for dc in range(2)])
                    for hb in range(2):
                        M.reduce(mx(S_[:, 2 * hb:2 * hb + 2]),
                                 V(lb[hb].h[:, :].rearrange("p (h m) -> p h m", h=2), lb[hb].buf()), ALU.max)
                    M.ts(nmx(), mx(), -1.0, ALU.mult)
                    for h in range(4):
                        M.activation(pf(S_[:, h, :]), lb[h // 2](S_[:, (h % 2) * 256:(h % 2 + 1) * 256]), AF.Exp,
                                     bias=nmx(S_[:, h:h + 1]), accum=rsum(S_[:, h:h + 1]))
                    M.recip(rinv(), rsum())
                    M.tt(pn(), pf(), V(bc_last(rinv.h[:, 0:4], [128, 4, 256]), rinv.buf()), ALU.mult)
                    bank = psn()
                    for h in range(4):
                        for mt in range(2):
                            i = h * 2 + mt
                            M.transpose(pbf(bank, i * 128, (i + 1) * 128), pn(S_[:, h, mt * 128:(mt + 1) * 128]),
                                        identb())
                    M.copy(pT(), pbf(bank, 0, 1024), E=M.act)
                    for j in range(2):
                        bank = psn()
                        for i4 in range(4):
                            idx = j * 4 + i4
                            h, dc = divmod(idx, 2)
                            M.mm(bank(S_[:, i4 * 128:(i4 + 1) * 128]),
                                 [(vsb(S_[:, mt, h * 256 + dc * 128:h * 256 + (dc + 1) * 128]),
                                   pT(S_[:, (h * 2 + mt) * 128:(h * 2 + mt + 1) * 128])) for mt in range(2)])
                        M.tt(V(omT.h[:, j * 4:(j + 1) * 4, qsl], omT.buf()),
                             V(bank.h[:, :].rearrange("p (c q) -> p c q", c=4), bank.buf()),
                             V(szmT.h[:, j * 4:(j + 1) * 4, qsl], szmT.buf()), ALU.mult)
                M.barrier()
            M.es = es12
            if stop == "c1":
                return nc
            with ExitStack() as es2:
                M.es = es2
                oTa = M.sb([128, 24, 1024], BF16, "oTa")
                wfb = [M.sb([128, 80, 128], BF16, "wfb%d" % i) for i in range(2)]
                gtmp = [M.sb([128, 512], F32, "gtmp%d" % i) for i in range(3)]
                mt_ = [M.sb([128, 512], F32, "mtm%d" % i) for i in range(3)]
                emit_ag(12)
                for i in range(24):
                    M._deps(M.pool, [oTa(key=i)], [V(None, Gbuf), gidx()])
                    slot = M.next_slot(M.pool)
                    inst = nc.gpsimd.indirect_dma_start(
                        out=oTa.h[:, i, :], out_offset=None, in_=G_q[i // 2].ap()[:, :],
                        in_offset=bass.IndirectOffsetOnAxis(ap=gidx.h[:, i % 2:i % 2 + 1], axis=0))
                    slot[1] += 16
                    inst.then_inc(slot[0], 16)
                    M._mark((slot[0], slot[1]), [oTa(key=i)], [V(None, Gbuf), gidx()])
                for f in range(16):
                    wt = wfb[f % 2]
                    for q5 in range(5):
                        M.dma(M.pool, wt.h[:, q5 * 16:(q5 + 1) * 16, :], wf_ds[f][:, q5 * 16:(q5 + 1) * 16, :],
                              outs=[wt(key=q5)])
                    wv = lambda r: wt(S_[:, r, :], key=r // 16)
                    for grp in range(2):
                        gsl = slice(grp * 512, (grp + 1) * 512)
                        for br in range(3):
                            bank = psn()
                            M.mm(bank(), [(wv(br * 16 + kc), hTo(S_[:, kc, gsl], key=kc)) for kc in range(KC)])
                            M.activation(gtmp[br](), bank(), AF.Exp, bias=negbg(S_[:, br * 16 + f:br * 16 + f + 1]),
                                         scale=-1.0)
                            M.ts(gtmp[br](), gtmp[br](), 1.0, ALU.add)
                            M.recip(gtmp[br](), gtmp[br]())
                        bank = psn()
                        M.mm(bank(), [(wv(48 + kc), oTa(S_[:, kc, gsl], key=kc)) for kc in range(8)])
                        M.tt(mt_[0](), bank(), gtmp[0](), ALU.mult)
                        bank = psn()
                        M.mm(bank(), [(wv(56 + k2), oTa(S_[:, 8 + k2, gsl], key=8 + k2)) for k2 in range(16)])
                        M.tt(mt_[1](), bank(), gtmp[1](), ALU.mult)
                        bank = psn()
                        M.mm(bank(), [(wv(72 + kc), V(omT.h[:, kc, gsl], omT.buf())) for kc in range(8)])
                        M.tt(mt_[2](), bank(), gtmp[2](), ALU.mult)
                        M.tt(mt_[0](), mt_[0](), mt_[1](), ALU.add)
                        M.tt(V(mergedT.h[:, f, gsl], mergedT.buf(f)), mt_[0](), mt_[2](), ALU.add)
                M.barrier()
            M.es = esc

            if stop == "c2":
                return nc
            es12.close()
            with ExitStack() as es3:
                M.es = es3
                xres = M.sb([128, 8, D], F32, "xres")
                wob = [M.sb([128, KC, 512], BF16, "wob%d" % i) for i in range(2)]
                fnw = M.sb([128, D], F32, "fnw")
                ybuf = [M.sb([128, D], F32, "ybuf%d" % i) for i in range(2)]
                junk3 = M.sb([128, D], BF16, "junk3")
                fss = M.sb([128, 8], F32, "fss")
                fln = M.sb([128, 8], F32, "fln")
                frs = M.sb([128, 8], F32, "frs")
                M.dma(M.sp, fnw.h[:], fnw_d[:, :], outs=[fnw()])
                for tt_ in range(8):
                    M.dma(M.sp, xres.h[:, tt_, :], xo_d[tt_ * 128:(tt_ + 1) * 128, :], outs=[xres(key=tt_)])
                for cbk in range(4):
                    t = wob[cbk % 2]
                    for q4 in range(4):
                        M.dma(M.pool, t.h[:, q4 * 4:(q4 + 1) * 4, :],
                              wo_d[:, q4 * 4:(q4 + 1) * 4, cbk * 512:(cbk + 1) * 512], outs=[t(key=q4)])
                    for tt_ in range(8):
                        bank = psn()
                        M.mm(bank(), [(V(mergedT.h[:, f, tt_ * 128:(tt_ + 1) * 128], mergedT.buf(f)),
                                       t(S_[:, f, :], key=f // 4)) for f in range(16)])
                        rs_ = xres(S_[:, tt_, cbk * 512:(cbk + 1) * 512], key=tt_)
                        M.tt(rs_, bank(), rs_, ALU.add)
                out_evs = []
                for tt_ in range(8):
                    M.activation(junk3(), xres(S_[:, tt_, :], key=tt_), AF.Square, accum=fss(S_[:, tt_:tt_ + 1]))
                rstd_of(frs(), fss(), D, fln())
                for tt_ in range(8):
                    yb = ybuf[tt_ % 2]
                    M.stt(yb(), xres(S_[:, tt_, :], key=tt_), frs(S_[:, tt_:tt_ + 1]), fnw(), ALU.mult, ALU.mult)
                    out_evs.append(M.dma(M.sp, y_d[tt_ * 128:(tt_ + 1) * 128, :], yb.h[:], ins=[yb()]))
                for ev in out_evs:
                    M.wait(M.sp, ev)
        M.es = es
    return nc


def _prep_inputs(x, mem, norm_w, mem_norm_w, w_in, b_gate, conv_w, conv_b, dt_bias, a_log, d_skip, ssd_norm_w,
                 w_mem_kv, w_branch_sb, w_branch_ssd, w_branch_mem, w_out, final_norm_w):
    f32 = np.float32
    x = np.asarray(x, f32)
    mem = np.asarray(mem, f32)
    w_in = np.asarray(w_in, f32)[0]
    xf = np.ascontiguousarray(x.reshape(TOK, D))
    nw = np.asarray(norm_w, f32)[0]
    mnw = np.asarray(mem_norm_w, f32)[0]
    bg = np.asarray(b_gate, f32)[0]
    cwt = np.asarray(conv_w, f32)[0]
    cbs = np.asarray(conv_b, f32)[0]
    dtb = np.asarray(dt_bias, f32)[0]
    alog = np.asarray(a_log, f32)[0]
    dsk = np.asarray(d_skip, f32)[0]
    snw = np.asarray(ssd_norm_w, f32)[0]
    wkv = np.asarray(w_mem_kv, f32)[0]
    wbsb = np.asarray(w_branch_sb, f32)[0]
    wbssd = np.asarray(w_branch_ssd, f32)[0]
    wbmem = np.asarray(w_branch_mem, f32)[0]
    wo = np.asarray(w_out, f32)[0]
    fnw = np.asarray(final_norm_w, f32)

    def pk(w):
        k = w.shape[0] // 128
        return np.ascontiguousarray(w.reshape(k, 128, w.shape[1]).transpose(1, 0, 2))

    wq_h = pk(w_in[:, 10272:10272 + 2048])
    wkv_h = pk(wkv)
    wo_h = pk(wo)
    gate = w_in[:, 12320:12320 + 6144]
    perm_sb = np.array([(h * 4 + rr) * 128 + q * 32 + i for q in range(4) for h in range(2) for rr in range(4)
                        for i in range(32)])
    perm_ssd = np.array([(h * 4 + rr) * 256 + q * 32 + i for q in range(8) for h in range(2) for rr in range(4)
                         for i in range(32)])
    wf_h = np.empty((16, 128, 80, 128), f32)
    for f in range(16):
        cs = slice(f * 128, (f + 1) * 128)
        parts = [gate[:, 0 * 2048 + f * 128:0 * 2048 + (f + 1) * 128],
                 gate[:, 1 * 2048 + f * 128:1 * 2048 + (f + 1) * 128],
                 gate[:, 2 * 2048 + f * 128:2 * 2048 + (f + 1) * 128],
                 wbsb[perm_sb][:, cs], wbssd[perm_ssd][:, cs], wbmem[:, cs]]
        allr = np.concatenate(parts, axis=0)
        wf_h[f] = allr.reshape(80, 128, 128).transpose(1, 0, 2)
    vc = np.zeros((128, 80), f32)
    vc[:, 0:48] = bg.reshape(48, 128).T
    vc[:, 48:64] = nw.reshape(16, 128).T
    vc[:, 64:80] = mnw.reshape(16, 128).T
    fnw_h = np.ascontiguousarray(np.broadcast_to(fnw[None, :], (128, D)))
    ones = np.ones((128, 128), f32)
    cst = np.concatenate([np.eye(128, dtype=f32), np.triu(ones), ones, -np.tril(ones), -ones, np.triu(ones, 1)],
                         axis=1).astype(f32)
    maps = []
    for c in range(NCORES):
        cols = np.concatenate([
            np.arange(c * 128, (c + 1) * 128),
            1024 + np.arange(c * 128, (c + 1) * 128),
            3072 + np.arange(c * 128, (c + 1) * 128),
            6144 + np.arange(c * 256, (c + 1) * 256),
            8192 + np.arange(c * 128, (c + 1) * 128),
            9216 + np.arange(c * 128, (c + 1) * 128),
            2048 + np.arange(c * 128, (c + 1) * 128),
            4096 + np.arange(c * 256, (c + 1) * 256),
            10240 + np.arange(c * 4, (c + 1) * 4),
        ])
        wa_h = pk(w_in[:, cols])
        va = np.zeros((128, 304), f32)
        va[:, 0:16] = nw.reshape(16, 128).T
        chans = [c * 256 + np.arange(128), c * 256 + 128 + np.arange(128), 2048 + c * 128 + np.arange(128),
                 3072 + c * 128 + np.arange(128)]
        for ch in range(4):
            for k in range(4):
                va[:, 16 + ch * 4 + k] = cwt[k, chans[ch]]
            va[:, 32 + ch] = cbs[chans[ch]]
        va[:, 36:40] = dtb[c * 4:(c + 1) * 4][None, :]
        va[:, 40:44] = alog[c * 4:(c + 1) * 4][None, :]
        va[:, 44:48] = dsk[c * 4:(c + 1) * 4][None, :]
        va[:, 48:304] = snw[c * 256:(c + 1) * 256][None, :]
        gidx = np.zeros((128, 2), np.int32)
        for h in range(2):
            for rr in range(4):
                gidx[rr * 32:(rr + 1) * 32, h] = (h * 4 + rr) * (NCORES * 32) + c * 32 + np.arange(32)
        mp = {"wf%d" % i: wf_h[i] for i in range(16)}
        maps.append(mp)
        mp.update({
            "x0": xf[0:4096], "x1": xf[4096:8192], "xo": np.ascontiguousarray(xf[c * 1024:(c + 1) * 1024]),
            "mem": np.ascontiguousarray(mem[c // 4]),
            "wa": wa_h, "wq": wq_h, "wkv": wkv_h, "wo": wo_h,
            "va": va, "vc": vc, "fnw": fnw_h, "cst": cst, "gidx": gidx,
        })
    return maps


def kernel(**inputs):
    maps = _prep_inputs(**inputs)
    nc = build()
    res = run_bass_kernel_spmd(nc, maps, core_ids=list(range(NCORES)))
    out = np.concatenate([np.asarray(res.results[c]["y"], np.float32) for c in range(NCORES)], axis=0)
    return out.reshape(2, 4096, D)
```
